# Optimizing a Trainium2 kernel written in Bass

```python
import jax, jax.numpy as jnp
from jax import lax
import numpy as np

D_MODEL = 1024
BATCH = 8
SEQ = 4096
DEPTH = 2

EXPAND = 2
D_INNER = EXPAND * D_MODEL
N_GROUPS = 8
GROUP_DIM = D_INNER // N_GROUPS
CHUNK = 128
N_MIXERS = 2
EPS = 1e-6

kernel_name = "hybrid_fourier_sgu_encoder"


def rms_norm(x, g):
    xf = x.astype(jnp.float32)
    y = xf * lax.rsqrt(jnp.mean(xf * xf, axis=-1, keepdims=True) + EPS)
    return (y * g.astype(jnp.float32)).astype(x.dtype)


def layer_norm(x, g, b):
    xf = x.astype(jnp.float32)
    mu = jnp.mean(xf, axis=-1, keepdims=True)
    var = jnp.mean(jnp.square(xf - mu), axis=-1, keepdims=True)
    y = (xf - mu) * lax.rsqrt(var + EPS)
    return (y * g.astype(jnp.float32) + b.astype(jnp.float32)).astype(x.dtype)


def fourier_mixer(h, w_in, w_out):
    b, s, _ = h.shape
    proj = h @ w_in
    xin, z = proj[..., :D_INNER], proj[..., D_INNER:]
    xg = xin.reshape(b, s, N_GROUPS, GROUP_DIM).astype(jnp.float32)
    y = jnp.fft.fft2(xg, axes=(1, 3), norm="ortho").real
    y = y.astype(h.dtype).reshape(b, s, D_INNER)
    return (y * jax.nn.silu(z)) @ w_out


def sgu_mixer(h, w_in, v_ln_g, v_ln_b, w_s, b_s, w_out):
    b, s, _ = h.shape
    proj = h @ w_in
    uv = jax.nn.gelu(proj[..., :2 * D_INNER])
    z = proj[..., 2 * D_INNER:]
    u, v = uv[..., :D_INNER], uv[..., D_INNER:]
    v = layer_norm(v, v_ln_g, v_ln_b)
    v = v.reshape(b, s // CHUNK, CHUNK, N_GROUPS, GROUP_DIM)
    v = jnp.einsum('gpq,bnqgc->bnpgc', w_s, v) + b_s.T[None, None, :, :, None]
    y = u * v.reshape(b, s, D_INNER)
    return (y * jax.nn.silu(z)) @ w_out


def setup_inputs(seed: int = 0) -> dict:
    key = jax.random.key(seed)
    ks = jax.random.split(key, 12)
    f32 = jnp.float32
    d, e = D_MODEL, D_INNER
    return {
        "x": jax.random.normal(ks[0], (BATCH, SEQ, d), f32),
        "l0_norm": 1.0 + 0.02 * jax.random.normal(ks[1], (d,), f32),
        "l0_w_in": jax.random.normal(ks[2], (d, 2 * e), f32) * d ** -0.5,
        "l0_w_out": jax.random.normal(ks[3], (e, d), f32) * e ** -0.5,
        "l1_norm": 1.0 + 0.02 * jax.random.normal(ks[4], (d,), f32),
        "l1_w_in": jax.random.normal(ks[5], (d, 3 * e), f32) * d ** -0.5,
        "l1_v_ln_g": 1.0 + 0.02 * jax.random.normal(ks[6], (e,), f32),
        "l1_v_ln_b": 0.02 * jax.random.normal(ks[7], (e,), f32),
        "l1_w_s": jax.random.normal(ks[8], (N_GROUPS, CHUNK, CHUNK), f32) * CHUNK ** -0.5,
        "l1_b_s": 1.0 + 0.02 * jax.random.normal(ks[9], (N_GROUPS, CHUNK), f32),
        "l1_w_out": jax.random.normal(ks[10], (e, d), f32) * e ** -0.5,
        "final_norm": 1.0 + 0.02 * jax.random.normal(ks[11], (d,), f32),
    }


def reference(x, l0_norm, l0_w_in, l0_w_out, l1_norm, l1_w_in, l1_v_ln_g, l1_v_ln_b,
              l1_w_s, l1_b_s, l1_w_out, final_norm):
    mixers = [fourier_mixer, sgu_mixer]
    norms = [l0_norm, l1_norm]
    params = [
        (l0_w_in, l0_w_out),
        (l1_w_in, l1_v_ln_g, l1_v_ln_b, l1_w_s, l1_b_s, l1_w_out),
    ]
    for i in range(DEPTH):
        h = rms_norm(x, norms[i])
        x = x + mixers[i % N_MIXERS](h, *params[i])
    return rms_norm(x, final_norm)
```

```python
import numpy as np
import ml_dtypes
from contextlib import ExitStack
import concourse.bass as bass
import concourse.mybir as mybir
from concourse.bass_utils import run_bass_kernel_spmd

F32 = mybir.dt.float32
BF16 = mybir.dt.bfloat16
AF = mybir.ActivationFunctionType
ALU = mybir.AluOpType

S = 4096
D = 1024
E = 2048
NT = 32
EPS = 1e-6
NCORES = 8


def _host_consts():
    bf = ml_dtypes.bfloat16
    c = np.arange(256)[:, None]
    m = np.arange(256)[None, :]
    ang = 2 * np.pi * ((c * m) % 256) / 256.0
    Cc, Sc = np.cos(ang), np.sin(ang)
    FC = np.zeros((128, 2, 2, 256), np.float64)
    for kc in range(2):
        for mh in range(2):
            cs = slice(kc * 128, (kc + 1) * 128)
            ms = slice(mh * 128, (mh + 1) * 128)
            FC[:, kc, mh, 0:128] = Cc[cs, ms]
            FC[:, kc, mh, 128:256] = -Sc[cs, ms]
    p = np.arange(128)[:, None, None]
    t = np.arange(32)[None, :, None]
    kp = np.arange(128)[None, None, :]
    th = 2 * np.pi * ((32 * p * kp + t * kp) % 4096) / 4096.0
    WR = np.cos(th)
    WS = np.sin(th)
    tt = np.arange(32)[:, None]
    kt = np.arange(32)[None, :]
    ph = 2 * np.pi * ((tt * kt) % 32) / 32.0
    BDc = np.zeros((128, 128))
    BDs = np.zeros((128, 128))
    for q in range(4):
        BDc[32 * q:32 * q + 32, q::4] = np.cos(ph)
        BDs[32 * q:32 * q + 32, q::4] = np.sin(ph)
    ident = np.eye(128)
    return {
        "c_fc": FC.astype(np.float32).astype(bf),
        "c_wr": WR.astype(np.float32).astype(bf),
        "c_ws": WS.astype(np.float32).astype(bf),
        "c_wsn": (-WS).astype(np.float32).astype(bf),
        "c_bdc": BDc.astype(np.float32).astype(bf),
        "c_bds": BDs.astype(np.float32).astype(bf),
        "c_id": ident.astype(np.float32).astype(bf),
    }


class Buf:
    __slots__ = ("name", "w", "r")

    def __init__(self, name):
        self.name = name
        self.w = None
        self.r = {}


class DSem:
    def __init__(self, sem):
        self.sem = sem
        self.n = 0


class Q:
    def __init__(self, eng, sem, kind):
        self.eng = eng
        self.sem = sem
        self.n = 0
        self.seen = {}
        self.kind = kind
        self.load = 0.0

    def wait(self, tok):
        if tok is None:
            return
        sem, val = tok
        if sem is self.sem and self.kind == "pe":
            return
        k = id(sem)
        if self.seen.get(k, 0) >= val:
            return
        self.eng.wait_ge(sem, val)
        self.seen[k] = val

    def deps(self, reads, writes, waw=True):
        for b in reads:
            self.wait(b.w)
        for b in writes:
            if waw:
                self.wait(b.w)
            for t in list(b.r.values()):
                self.wait(t)

    def op(self, fn, reads=(), writes=(), waw=True, cost=0.3, inc=True):
        self.load += cost
        self.deps(reads, writes, waw)
        inst = fn(self.eng)
        if inc:
            self.n += 1
            inst.then_inc(self.sem, 1)
            tok = (self.sem, self.n)
        else:
            tok = (self.sem, self.n + 1)
        for b in reads:
            b.r[id(self.sem)] = tok
        for b in writes:
            b.w = tok
            b.r = {}
        return tok

    def dma(self, out, in_, dsem, reads=(), writes=(), **kw):
        self.deps(reads, writes, waw=False)
        self.eng.dma_start(out=out, in_=in_, **kw).then_inc(dsem.sem, 16)
        dsem.n += 16
        tok = (dsem.sem, dsem.n)
        for b in reads:
            b.r[id(dsem.sem)] = tok
        for b in writes:
            b.w = tok
            b.r = {}
        return tok


class Ctx:
    pass


def build(mode):
    nc = bass.Bass("TRN2", target_bir_lowering=False)
    K = Ctx()
    K.nc = nc
    dt = nc.dram_tensor

    def din(name, shape, dtype=F32):
        return dt(name, shape, dtype, kind="ExternalInput").ap()

    x = din("x", [S, D])
    g0 = din("l0_norm", [D])
    w_in0 = din("l0_w_in", [D, 2 * E])
    w_out0 = din("l0_w_out", [E, D])
    g1 = din("l1_norm", [D])
    w_in1 = din("l1_w_in", [D, 3 * E])
    vg = din("l1_v_ln_g", [E])
    vb = din("l1_v_ln_b", [E])
    w_s = din("l1_w_s", [8, 128, 128])
    b_s = din("l1_b_s", [8, 128])
    w_out1 = din("l1_w_out", [E, D])
    gF = din("final_norm", [D])
    c_fc = din("c_fc", [128, 2, 2, 256], BF16)
    c_wr = din("c_wr", [128, 32, 128], BF16)
    c_ws = din("c_ws", [128, 32, 128], BF16)
    c_wsn = din("c_wsn", [128, 32, 128], BF16)
    c_bdc = din("c_bdc", [128, 128], BF16)
    c_bds = din("c_bds", [128, 128], BF16)
    c_id = din("c_id", [128, 128], BF16)

    if mode == "A":
        gyT = dt("gy", [E, S], BF16, kind="ExternalOutput").ap()
    elif mode == "B":
        gyT = dt("gy", [E, S], BF16, kind="ExternalInput").ap()
    else:
        gyT = dt("gy", [E, S], BF16, kind="Internal").ap()
    if mode != "A":
        out = dt("out", [S, D], F32, kind="ExternalOutput").ap()
        x1d = dt("x1d", [S, D], F32, kind="Internal").ap()
        h1d = dt("h1d", [NT, 128, 8, 128], BF16, kind="Internal").ap()
    if mode != "B":
        Ud = dt("ud", [2, 128, 32, 256], BF16, kind="Internal").ap()
    if mode != "A":
        W0d = dt("w0d", [E, D], BF16, kind="Internal").ap()
        Wind = dt("wind", [D, 3 * E], BF16, kind="Internal").ap()
        W1d = dt("w1d", [E, D], BF16, kind="Internal").ap()

    with ExitStack() as es:
        def sem(name):
            return es.enter_context(nc.semaphore(name))

        pe = Q(nc.tensor, sem("s_pe"), "pe")
        act = Q(nc.scalar, sem("s_act"), "act")
        dve = Q(nc.vector, sem("s_dve"), "dve")
        pool = Q(nc.gpsimd, sem("s_pool"), "pool")
        sp = Q(nc.sync, sem("s_sp"), "sp")
        queues = [pe, act, dve, pool, sp]
        dsems = []

        def dsem(name):
            d = DSem(sem(name))
            dsems.append(d)
            return d

        def barrier():
            for q in queues:
                for r in queues:
                    if r is not q and r.n > 0:
                        q.wait((r.sem, r.n))
                for d in dsems:
                    if d.n > 0:
                        q.wait((d.sem, d.n))

        pf = [es.enter_context(nc.psum_tensor(f"pf{i}", [128, 512], F32)) for i in range(6)]
        pb = [es.enter_context(nc.psum_tensor(f"pb{i}", [128, 1024], BF16)) for i in range(2)]
        pfb = [Buf(f"pf{i}") for i in range(6)]
        pbb = [Buf(f"pb{i}") for i in range(2)]
        cnt = {"f": 0, "b": 0}

        def next_f():
            i = cnt["f"] % 6
            cnt["f"] += 1
            return pf[i], pfb[i]

        def next_b():
            i = cnt["b"] % 2
            cnt["b"] += 1
            return pb[i], pbb[i]

        def ca(n):
            return 0.19 + n / 1200.0

        def cd(n):
            return 0.12 + n / 960.0

        def ev_engine():
            return act if act.load <= dve.load else dve

        def copy_on(q, out_ap, in_ap, reads, writes, n=512):
            if q is act:
                return q.op(lambda e: e.activation(out=out_ap, in_=in_ap, func=AF.Copy), reads, writes,
                            waw=False, cost=ca(n))
            return q.op(lambda e: e.tensor_copy(out=out_ap, in_=in_ap), reads, writes, waw=False, cost=cd(n))

        rtmp = es.enter_context(nc.sbuf_tensor("rtmp", [128, 2], F32))
        rtmpb = Buf("rtmp")

        def rsqrt_pre(src_ap, srcb, mul, eps):
            dve.op(lambda e: e.tensor_scalar(out=rtmp[:, 0:1], in0=src_ap, scalar1=mul, scalar2=eps,
                                             op0=ALU.mult, op1=ALU.add), [srcb], [rtmpb], cost=0.15)
            dve.op(lambda e: e.reciprocal(out=rtmp[:, 1:2], in_=rtmp[:, 0:1]), [rtmpb], [rtmpb], cost=0.15)

        def rsqrt_post(rstd_t, rstdb):
            act.op(lambda e: e.activation(out=rstd_t[:], in_=rtmp[:, 1:2], func=AF.Sqrt), [rtmpb], [rstdb], cost=1.5)

        def rsqrt_chain(src_ap, srcb, rstd_t, rstdb, mul, eps):
            rsqrt_pre(src_ap, srcb, mul, eps)
            rsqrt_post(rstd_t, rstdb)

        def rstd_from_ss(ss_t, rstd_t, ssb, rstdb, n, eps):
            rsqrt_chain(ss_t[:], ssb, rstd_t, rstdb, 1.0 / n, eps)

        ident = es.enter_context(nc.sbuf_tensor("ident", [128, 128], BF16))
        identb = Buf("ident")
        d_const = dsem("d_const")
        sp.dma(ident[:], c_id[:, :], d_const, writes=[identb])

        g0T = es.enter_context(nc.sbuf_tensor("g0T", [128, 8], F32))
        g1T = es.enter_context(nc.sbuf_tensor("g1T", [128, 8], F32))
        bsT = es.enter_context(nc.sbuf_tensor("bsT", [128, 8], F32))
        cG = Buf("constsG")
        d_cg = dsem("d_cg")
        pool.dma(g0T[:], g0.rearrange("(dc p) -> p dc", p=128), d_cg, writes=[cG], allow_slow_non_contiguous=True)
        pool.dma(g1T[:], g1.rearrange("(dc p) -> p dc", p=128), d_cg, writes=[cG], allow_slow_non_contiguous=True)
        pool.dma(bsT[:], b_s.rearrange("g p -> p g"), d_cg, writes=[cG], allow_slow_non_contiguous=True)

        if mode != "A":
            d_pc = dsem("d_pc")
            W0db = Buf("W0d")
            Windb = [Buf(f"Wind{i}") for i in range(12)]
            W1db = Buf("W1d")
            pc_order = [4, 5, 6, 7, 0, 1, 2, 3, 8, 9, 10, 11]

            def precast(i):
                if i < 2:
                    pool.dma(W0d[1024 * i:1024 * (i + 1), :], w_out0[1024 * i:1024 * (i + 1), :], d_pc,
                             writes=[W0db])
                elif i < 14:
                    fb = pc_order[i - 2]
                    pool.dma(Wind[:, fb * 512:(fb + 1) * 512], w_in1[:, fb * 512:(fb + 1) * 512], d_pc,
                             writes=[Windb[fb]])
                else:
                    h = i - 14
                    pool.dma(W1d[1024 * h:1024 * (h + 1), :], w_out1[1024 * h:1024 * (h + 1), :], d_pc,
                             writes=[W1db])

        if mode != "B":
            with ExitStack() as sa:
                def sb(name, shape, dtype):
                    return sa.enter_context(nc.sbuf_tensor(name, shape, dtype))

                hT = sb("hT", [128, 8, S], BF16)
                hTb = Buf("hT")
                fc = sb("fc", [128, 2, 2, 256], BF16)
                wr = sb("wr", [128, 32, 128], BF16)
                ws = sb("ws", [128, 32, 128], BF16)
                wsneg = sb("wsneg", [128, 32, 128], BF16)
                bdc = sb("bdc", [128, 128], BF16)
                bds = sb("bds", [128, 128], BF16)
                cb = Buf("constsA")
                sp.dma(fc[:], c_fc[:, :, :, :], d_const, writes=[cb])
                sp.dma(wr[:], c_wr[:, :, :], d_const, writes=[cb])
                sp.dma(ws[:], c_ws[:, :, :], d_const, writes=[cb])
                sp.dma(wsneg[:], c_wsn[:, :, :], d_const, writes=[cb])
                sp.dma(bdc[:], c_bdc[:, :], d_const, writes=[cb])
                sp.dma(bds[:], c_bds[:, :], d_const, writes=[cb])
                identb.w = (d_const.sem, d_const.n)
                cb.w = (d_const.sem, d_const.n)

                wg = [sb(f"wg{i}", [128, 8, 512], BF16) for i in range(2)]
                wgb = [Buf(f"wg{i}") for i in range(2)]
                d_wg = [dsem(f"d_wg{i}") for i in range(2)]
                def load_wg(g):
                    sl = g % 2
                    pool.dma(wg[sl][:, :, 0:256],
                             w_in0[:, g * 256:(g + 1) * 256].rearrange("(dc p) n -> p dc n", p=128),
                             d_wg[sl], writes=[wgb[sl]])
                    pool.dma(wg[sl][:, :, 256:512],
                             w_in0[:, E + g * 256:E + (g + 1) * 256].rearrange("(dc p) n -> p dc n", p=128),
                             d_wg[sl], writes=[wgb[sl]])

                xinT = sb("xinT", [128, 2, S], BF16)
                xinTbs = [[Buf(f"xinT{kc}_{tb}") for tb in range(8)] for kc in range(2)]
                szT = sb("szT", [128, 2, S], BF16)
                szTb = Buf("szT")
                hTbs = [Buf(f"hT{tb}") for tb in range(8)]

                def proj(g, which, tbs=range(8)):
                    sl = g % 2
                    for kc in range(2):
                        for tb in tbs:
                            bank, bb = next_f()
                            for dc in range(8):
                                pe.op(lambda e: e.matmul(
                                    bank[:], wg[sl][:, dc, which * 256 + kc * 128: which * 256 + (kc + 1) * 128],
                                    hT[:, dc, tb * 512:(tb + 1) * 512], start=(dc == 0), stop=(dc == 7)),
                                    [wgb[sl], hTbs[tb]], [bb], waw=False, inc=(dc == 7))
                            if which == 0:
                                copy_on(ev_engine(), xinT[:, kc, tb * 512:(tb + 1) * 512], bank[:], [bb],
                                        [xinTbs[kc][tb]])
                            else:
                                o = szT[:, kc, tb * 512:(tb + 1) * 512]
                                act.op(lambda e: e.activation(out=o, in_=bank[:], func=AF.Silu), [bb], [szTb],
                                       waw=False, cost=ca(512))

                load_wg(0)

                with ExitStack() as s0:
                    NX = 14
                    xt = [s0.enter_context(nc.sbuf_tensor(f"a0x{i}", [128, D], F32)) for i in range(NX)]
                    xtb = [Buf(f"a0x{i}") for i in range(NX)]
                    d_x6 = [dsem(f"d_a0x{i}") for i in range(6)]
                    d_x = [d_x6[i % 6] for i in range(NX)]
                    junk = s0.enter_context(nc.sbuf_tensor("a0junk", [128, D], BF16))
                    junkb = Buf("junk")
                    xn = [s0.enter_context(nc.sbuf_tensor(f"a0xn{i}", [128, D], BF16)) for i in range(2)]
                    xnb = [Buf(f"a0xn{i}") for i in range(2)]
                    ss = [s0.enter_context(nc.sbuf_tensor(f"a0ss{i}", [128, 1], F32)) for i in range(2)]
                    ssb = [Buf(f"a0ss{i}") for i in range(2)]
                    rs = [s0.enter_context(nc.sbuf_tensor(f"a0rs{i}", [128, 1], F32)) for i in range(2)]
                    rsb = [Buf(f"a0rs{i}") for i in range(2)]
                    ssA = s0.enter_context(nc.sbuf_tensor("a0ssA", [128, NT], F32))
                    ssAb = Buf("ssA")
                    rsA = s0.enter_context(nc.sbuf_tensor("a0rsA", [128, NT], F32))
                    rsAb = Buf("rsA")
                    t32 = s0.enter_context(nc.sbuf_tensor("a0t32", [128, 2 * NT], F32))
                    t32b = Buf("t32")
                    def a0_load(ti, sx):
                        if d_x[sx].n > 0:
                            sp.wait((d_x[sx].sem, d_x[sx].n))
                        sp.dma(xt[sx][:], x[ti * 128:(ti + 1) * 128, :], d_x[sx], writes=[xtb[sx]])

                    def a0_sq(ti, sx):
                        act.op(lambda e: e.activation(out=junk[:], in_=xt[sx][:], func=AF.Square,
                                                      accum_out=ssA[:, ti:ti + 1]),
                               [xtb[sx]], [junkb, ssAb], waw=True, cost=ca(1024))

                    NE = NT - NX
                    for ti in range(NX, NT):
                        a0_load(ti, ti % NX)
                        a0_sq(ti, ti % NX)
                    for ti in range(0, NX):
                        a0_load(ti, ti)
                        a0_sq(ti, ti)
                    dve.op(lambda e: e.tensor_scalar(out=t32[:, 0:NT], in0=ssA[:], scalar1=1.0 / D, scalar2=EPS,
                                                     op0=ALU.mult, op1=ALU.add), [ssAb], [t32b])
                    dve.op(lambda e: e.reciprocal(out=t32[:, NT:2 * NT], in_=t32[:, 0:NT]), [t32b], [t32b])
                    act.op(lambda e: e.activation(out=rsA[:], in_=t32[:, NT:2 * NT], func=AF.Sqrt), [t32b], [rsAb])
                    for n2 in range(NT):
                        ti = n2
                        sx = ti % NX
                        sl = n2 % 2
                        if True:
                            act.op(lambda e: e.activation(out=xn[sl][:], in_=xt[sx][:], func=AF.Copy,
                                                          scale=rsA[:, ti:ti + 1]),
                                   [xtb[sx], rsAb], [xnb[sl]], cost=ca(1024))
                        else:
                            dve.op(lambda e: e.tensor_scalar(out=xn[sl][:], in0=xt[sx][:],
                                                             scalar1=rsA[:, ti:ti + 1], scalar2=None,
                                                             op0=ALU.mult),
                                   [xtb[sx], rsAb], [xnb[sl]], cost=cd(512))
                        if ti + NX < NT:
                            a0_load(ti + NX, sx)
                        bank, bb = next_b()
                        for dc in range(8):
                            pe.op(lambda e: e.transpose(out=bank[:, dc * 128:(dc + 1) * 128],
                                                        in_=xn[sl][:, dc * 128:(dc + 1) * 128],
                                                        identity=ident[:]),
                                  [xnb[sl], identb], [bb], waw=False, inc=(dc == 7))
                        dve.op(lambda e: e.tensor_tensor(
                            out=hT[:, :, ti * 128:(ti + 1) * 128],
                            in0=bank[:].rearrange("p (a b) -> p a b", b=128),
                            in1=g0T[:].rearrange("p (k o) -> p k o", o=1).broadcast_to([128, 8, 128]),
                            op=ALU.mult), [bb, cb, cG], [hTbs[ti // 4]], waw=False, cost=cd(1024))
                        if ti % 4 == 3:
                            proj(0, 0, [ti // 4])
                        if ti % 8 == 7:
                            proj(0, 1, [ti // 4 - 1, ti // 4])
                barrier()

                AB = sb("AB", [128, 32, 256], BF16)
                ABbs = [Buf(f"AB{t}") for t in range(16)]
                U = sb("U", [128, 32, 256], BF16)
                Ubs = [Buf(f"U{t}") for t in range(16)]
                V = sb("V", [128, 32, 256], BF16)
                Vb = Buf("V_sp")
                Vb2 = Buf("V_pool")
                gyU = sb("gyU", [128, S], BF16)
                gyUb = Buf("gyU")
                Udb = [Buf("Ud0"), Buf("Ud1")]
                d_ud = dsem("d_ud")
                d_v = dsem("d_v")
                d_v2 = dsem("d_v2")
                d_gy = dsem("d_gy")
                gyTb = Buf("gyT")

                def p3(g, mh):
                    for t2 in range(16):
                        bank, bb = next_f()
                        for h in range(2):
                            t = 2 * t2 + h
                            for kc in range(2):
                                lhsT = xinT[:, kc, :].rearrange("c (p t) -> c t p", t=32)[:, t, :]
                                pe.op(lambda e: e.matmul(bank[:, h * 256:(h + 1) * 256], lhsT, fc[:, kc, mh, :],
                                                         start=(kc == 0), stop=(kc == 1)),
                                      xinTbs[kc] + [cb], [bb], waw=False, inc=(h == 1 and kc == 1))
                        copy_on(ev_engine(), AB[:, 2 * t2:2 * t2 + 2, :],
                                bank[:].rearrange("p (a b) -> p a b", b=256), [bb], [ABbs[t2]])

                def p4(g, mh, slot):
                    for t2 in range(16):
                        bank, bb = next_f()
                        for h in range(2):
                            t = 2 * t2 + h
                            c0 = h * 256
                            pe.op(lambda e: e.matmul(bank[:, c0:c0 + 256], wr[:, t, :], AB[:, t, 0:256],
                                                     start=True, stop=False), [ABbs[t2], cb], [bb], waw=False, inc=(False))
                            pe.op(lambda e: e.matmul(bank[:, c0:c0 + 128], ws[:, t, :], AB[:, t, 128:256],
                                                     start=False, stop=False), [ABbs[t2], cb], [bb], waw=False, inc=(False))
                            pe.op(lambda e: e.matmul(bank[:, c0 + 128:c0 + 256], wsneg[:, t, :], AB[:, t, 0:128],
                                                     start=False, stop=True), [ABbs[t2], cb], [bb], waw=False, inc=(h == 1))
                        copy_on(ev_engine(), U[:, 2 * t2:2 * t2 + 2, :],
                                bank[:].rearrange("p (a b) -> p a b", b=256), [bb], [Ubs[t2]])
                    sp.dma(Ud[slot], U[:], d_ud, reads=Ubs, writes=[Udb[slot]])

                def load_v(slot):
                    for q in range(4):
                        if q % 2 == 0:
                            sp.dma(V[32 * q:32 * q + 32, :, :],
                                   Ud[slot].rearrange("(k q) t c -> q t k c", q=4)[q],
                                   d_v, reads=[Udb[slot]], writes=[Vb])
                        else:
                            pool.dma(V[32 * q:32 * q + 32, :, :],
                                     Ud[slot].rearrange("(k q) t c -> q t k c", q=4)[q],
                                     d_v2, reads=[Udb[slot]], writes=[Vb2])

                def p6p7(g, mh):
                    for b in range(8):
                        bank, bb = next_f()
                        for j in range(4):
                            kpp = 4 * b + j
                            o = bank[:].rearrange("p (kt j q) -> p j kt q", j=4, q=4)[:, j, :, :]
                            pe.op(lambda e: e.matmul(o, V[:, kpp, 0:128], bdc[:].rearrange("p (kt q) -> p kt q", q=4),
                                                     start=True, stop=False), [Vb, Vb2, cb], [bb], waw=False, inc=(False))
                            pe.op(lambda e: e.matmul(o, V[:, kpp, 128:256], bds[:].rearrange("p (kt q) -> p kt q", q=4),
                                                     start=False, stop=True), [Vb, Vb2, cb], [bb], waw=False, inc=(j == 3))

                        def tokview(t2d):
                            return t2d.rearrange("p (kt b r) -> p b kt r", b=8, r=16)[:, b, :, :]
                        dve.op(lambda e: e.scalar_tensor_tensor(
                            out=tokview(gyU[:]),
                            in0=bank[:].rearrange("p (kt r) -> p kt r", r=16),
                            scalar=1.0 / 1024.0,
                            in1=tokview(szT[:, mh, :]),
                            op0=ALU.mult, op1=ALU.mult), [bb, szTb], [gyUb], waw=False, cost=cd(512) + 0.2)
                    c0 = g * 256 + mh * 128
                    sp.dma(gyT[c0:c0 + 128, :], gyU[:], d_gy, reads=[gyUb], writes=[gyTb])

                for g in range(8):
                    if g + 1 < 8:
                        load_wg(g + 1)
                    if mode == "AB":
                        precast(2 * g)
                        precast(2 * g + 1)
                    if g > 0:
                        proj(g, 0)
                        p6p7(g - 1, 1)
                    p3(g, 0)
                    p4(g, 0, 0)
                    load_v(0)
                    if g > 0:
                        proj(g, 1)
                    p6p7(g, 0)
                    p3(g, 1)
                    p4(g, 1, 1)
                    load_v(1)
                p6p7(7, 1)
                barrier()
        else:
            gyTb = Buf("gyT")

        if mode != "A":
            with ExitStack() as sB:
                def sb(name, shape, dtype):
                    return sB.enter_context(nc.sbuf_tensor(name, shape, dtype))

                Win = sb("Win", [128, 8, 3 * E], BF16)
                Winb = [Buf(f"Win{i}") for i in range(12)]
                gFt = sb("gFt", [128, D], F32)
                vgt = sb("vgt", [128, E], BF16)
                vbt = sb("vbt", [128, E], BF16)
                wsn = sb("wsn", [128, 8, 128], BF16)
                wsT = sb("wsT", [128, 8, 128], BF16)
                cB = Buf("constsB")
                wsnb = Buf("wsn")
                wsTb = Buf("wsT")
                d_cB = dsem("d_cB")
                d_w = dsem("d_w")
                d_w0 = dsem("d_w0")

                with ExitStack() as s0:
                    W0 = s0.enter_context(nc.sbuf_tensor("W0", [128, 16, D], BF16))
                    W0b = Buf("W0")
                    if mode == "B":
                        for i in range(16):
                            precast(i)
                    W0b2 = Buf("W0_pool")
                    d_w0b = dsem("d_w0b")
                    sp.dma(W0[:, 0:8, :], W0d[0:1024, :].rearrange("(kc p) n -> p kc n", p=128),
                           d_w0, reads=[W0db], writes=[W0b])
                    pool.dma(W0[:, 8:16, :], W0d[1024:2048, :].rearrange("(kc p) n -> p kc n", p=128),
                             d_w0b, reads=[W0db], writes=[W0b2])

                    x1 = [s0.enter_context(nc.sbuf_tensor(f"b0x{i}", [128, D], F32)) for i in range(3)]
                    x1b = [Buf(f"b0x{i}") for i in range(3)]
                    d_x = [dsem(f"d_b0x{i}") for i in range(3)]
                    gyt = [s0.enter_context(nc.sbuf_tensor(f"b0gy{i}", [128, 16, 512], BF16)) for i in range(2)]
                    gytb = [Buf(f"b0gy{i}") for i in range(2)]
                    d_g = [dsem(f"d_b0g{i}") for i in range(2)]
                    xn = [s0.enter_context(nc.sbuf_tensor(f"b0xn{i}", [128, D], BF16)) for i in range(2)]
                    xnb = [Buf(f"b0xn{i}") for i in range(2)]
                    h1 = [s0.enter_context(nc.sbuf_tensor(f"b0h{i}", [128, 8, 128], BF16)) for i in range(2)]
                    h1b = [Buf(f"b0h{i}") for i in range(2)]
                    ss = [s0.enter_context(nc.sbuf_tensor(f"b0ss{i}", [128, 1], F32)) for i in range(2)]
                    ssb = [Buf(f"b0ss{i}") for i in range(2)]
                    rs = [s0.enter_context(nc.sbuf_tensor(f"b0rs{i}", [128, 1], F32)) for i in range(2)]
                    rsb = [Buf(f"b0rs{i}") for i in range(2)]
                    d_x1 = dsem("d_x1d")
                    d_g0b = dsem("d_g0b")
                    gy0hb = Buf("gy0_pool_half")
                    d_h1 = dsem("d_h1d")
                    x1db = Buf("x1d")
                    h1db = Buf("h1d")

                    def b0_load_gy(jb):
                        sl = jb % 2
                        src = gyT[:, jb * 512:(jb + 1) * 512].rearrange("(kc p) s -> p kc s", p=128)
                        if jb == 0:
                            sp.dma(gyt[sl][:, 0:8, :], src[:, 0:8, :], d_g[sl], reads=[gyTb], writes=[gytb[sl]])
                            pool.dma(gyt[sl][:, 8:16, :], src[:, 8:16, :], d_g0b, reads=[gyTb], writes=[gy0hb])
                        else:
                            sp.dma(gyt[sl][:], src, d_g[sl], reads=[gyTb], writes=[gytb[sl]])

                    def b0_load_x(i):
                        sp.dma(x1[i % 3][:], x[i * 128:(i + 1) * 128, :], d_x[i % 3], writes=[x1b[i % 3]])

                    def b0_front(i):
                        sl = i % 2
                        s3_ = i % 3
                        jb = i // 4
                        gt = gyt[jb % 2]
                        gb = gytb[jb % 2]
                        gbs = [gb, gy0hb] if jb == 0 else [gb]
                        c0 = (i % 4) * 128
                        banks = []
                        for nh in range(2):
                            bank, bb = next_f()
                            banks.append((bank, bb))
                            for kc in range(16):
                                pe.op(lambda e: e.matmul(bank[:], gt[:, kc, c0:c0 + 128],
                                                         W0[:, kc, nh * 512:(nh + 1) * 512],
                                                         start=(kc == 0), stop=(kc == 15)),
                                      gbs + [W0b, W0b2], [bb], waw=False, inc=(kc == 15))
                        for nh in range(2):
                            bank, bb = banks[nh]
                            dve.op(lambda e: e.tensor_tensor(out=x1[s3_][:, nh * 512:(nh + 1) * 512],
                                                             in0=x1[s3_][:, nh * 512:(nh + 1) * 512], in1=bank[:],
                                                             op=ALU.add), [bb, x1b[s3_]], [x1b[s3_]], cost=cd(512))
                        act.op(lambda e: e.activation(out=xn[sl][:], in_=x1[s3_][:], func=AF.Square,
                                                      accum_out=ss[sl][:]), [x1b[s3_]], [xnb[sl], ssb[sl]],
                               cost=ca(1024))
                        rstd_from_ss(ss[sl], rs[sl], ssb[sl], rsb[sl], D, EPS)
                        act.op(lambda e: e.activation(out=xn[sl][:], in_=x1[s3_][:], func=AF.Copy,
                                                      scale=rs[sl][:, 0:1]), [x1b[s3_], rsb[sl]], [xnb[sl]],
                               cost=ca(1024))

                    def b0_back(i):
                        sl = i % 2
                        s3_ = i % 3
                        bank, bb = next_b()
                        for dc in range(8):
                            pe.op(lambda e: e.transpose(out=bank[:, dc * 128:(dc + 1) * 128],
                                                        in_=xn[sl][:, dc * 128:(dc + 1) * 128], identity=ident[:]),
                                  [xnb[sl], identb], [bb], waw=False, inc=(dc == 7))
                        dve.op(lambda e: e.tensor_tensor(
                            out=h1[sl][:], in0=bank[:].rearrange("p (a b) -> p a b", b=128),
                            in1=g1T[:].rearrange("p (k o) -> p k o", o=1).broadcast_to([128, 8, 128]),
                            op=ALU.mult), [bb, cG], [h1b[sl]], cost=cd(1024))
                        sp.dma(x1d[i * 128:(i + 1) * 128, :], x1[s3_][:], d_x1, reads=[x1b[s3_]], writes=[x1db])
                        sp.dma(h1d[i], h1[sl][:], d_h1, reads=[h1b[sl]], writes=[h1db])

                    b0_load_x(0)
                    b0_load_gy(0)
                    pool.dma(gFt[:], gF.partition_broadcast(128), d_cB, writes=[cB])
                    pool.dma(vgt[:], vg.partition_broadcast(128), d_cB, writes=[cB])
                    pool.dma(vbt[:], vb.partition_broadcast(128), d_cB, writes=[cB])
                    pool.dma(wsn[:], w_s.rearrange("g p q -> p g q"), d_cB, writes=[wsnb])
                    for fb in pc_order:
                        pool.dma(Win[:, :, fb * 512:(fb + 1) * 512],
                                 Wind[:, fb * 512:(fb + 1) * 512].rearrange("(kc p) n -> p kc n", p=128),
                                 d_w, reads=[Windb[fb]], writes=[Winb[fb]])
                    for i in range(NT + 1):
                        if i < NT:
                            if i % 4 == 0 and i // 4 + 1 < NT // 4:
                                b0_load_gy(i // 4 + 1)
                            if i + 1 < NT:
                                b0_load_x(i + 1)
                            b0_front(i)
                        if i >= 1:
                            b0_back(i - 1)
                    bank, bb = next_b()
                    for g in range(8):
                        pe.op(lambda e: e.transpose(out=bank[:, g * 128:(g + 1) * 128], in_=wsn[:, g, :],
                                                    identity=ident[:]), [wsnb, identb], [bb], waw=False, inc=(g == 7))
                    dve.op(lambda e: e.tensor_copy(out=wsT[:], in_=bank[:].rearrange("p (a b) -> p a b", b=128)),
                           [bb], [wsTb])
                barrier()

                with ExitStack() as s1:
                    def sb1(name, shape, dtype):
                        return s1.enter_context(nc.sbuf_tensor(name, shape, dtype))
                    NS = 2
                    W1 = sb1("W1", [128, 16, D], BF16)
                    W1b = Buf("W1")
                    x1 = [sb1(f"b1x{i}", [128, D], F32) for i in range(NS)]
                    x1b = [Buf(f"b1x{i}") for i in range(NS)]
                    d_x = [dsem(f"d_b1x{i}") for i in range(NS)]
                    h1 = [sb1(f"b1h{i}", [128, 8, 128], BF16) for i in range(NS)]
                    h1b = [Buf(f"b1h{i}") for i in range(NS)]
                    d_h = [dsem(f"d_b1h{i}") for i in range(NS)]
                    u = [sb1(f"b1u{i}", [128, E], BF16) for i in range(NS)]
                    ub = [Buf(f"b1u{i}") for i in range(NS)]
                    v = [sb1("b1v0", [128, E], F32)] * NS
                    vb_ = [Buf("b1v0")] * NS
                    sz = [sb1(f"b1sz{i}", [128, E], BF16) for i in range(NS)]
                    szb = [Buf(f"b1sz{i}") for i in range(NS)]
                    vln = [sb1(f"b1vln{i}", [128, E], BF16) for i in range(NS)]
                    vlnb = [Buf(f"b1vln{i}") for i in range(NS)]
                    g1t = [sb1(f"b1g1T{i}", [128, 16, 128], BF16) for i in range(NS)]
                    g1tb = [[Buf(f"b1g1T{i}_{h}") for h in range(2)] for i in range(NS)]
                    junk = sb1("b1junk", [128, D], BF16)
                    junkb = Buf("junk")
                    st = [sb1(f"b1st{i}", [128, 24], F32) for i in range(NS)]
                    stb = [Buf(f"b1st{i}") for i in range(NS)]
                    mv = [sb1(f"b1mv{i}", [128, 2], F32) for i in range(NS)]
                    mvb = [Buf(f"b1mv{i}") for i in range(NS)]
                    rv = [sb1(f"b1rv{i}", [128, 1], F32) for i in range(NS)]
                    rvb = [Buf(f"b1rv{i}") for i in range(NS)]
                    nb = [sb1(f"b1nb{i}", [128, 1], F32) for i in range(NS)]
                    nbb = [Buf(f"b1nb{i}") for i in range(NS)]
                    ss = [sb1(f"b1ss{i}", [128, 1], F32) for i in range(NS)]
                    ssb = [Buf(f"b1ss{i}") for i in range(NS)]
                    rs = [sb1(f"b1rs{i}", [128, 1], F32) for i in range(NS)]
                    rsb = [Buf(f"b1rs{i}") for i in range(NS)]
                    d_out = dsem("d_out")
                    outb = Buf("out")

                    def s3(i):
                        sl = i % NS

                        def block(fb):
                            bank, bb = next_f()
                            for kc in range(8):
                                pe.op(lambda e: e.matmul(bank[:], h1[sl][:, kc, :], Win[:, kc, fb * 512:(fb + 1) * 512],
                                                         start=(kc == 0), stop=(kc == 7)),
                                      [h1b[sl], Winb[fb]], [bb], waw=False, inc=(kc == 7))
                            return bank, bb

                        for fb in range(0, 4):
                            bank, bb = block(fb)
                            act.op(lambda e: e.activation(out=u[sl][:, fb * 512:(fb + 1) * 512], in_=bank[:],
                                                          func=AF.Gelu_apprx_tanh), [bb], [ub[sl]], waw=False,
                                   cost=ca(512))
                        for fb in range(4, 8):
                            j = fb - 4
                            bank, bb = block(fb)
                            act.op(lambda e: e.activation(out=v[sl][:, j * 512:(j + 1) * 512], in_=bank[:],
                                                          func=AF.Gelu_apprx_tanh), [bb], [vb_[sl]], waw=False,
                                   cost=ca(512))
                            dve.op(lambda e: e.bn_stats(out=st[sl][:, j * 6:(j + 1) * 6],
                                                        in_=v[sl][:, j * 512:(j + 1) * 512]),
                                   [vb_[sl]], [stb[sl]], waw=False, cost=cd(512))
                        dve.op(lambda e: e.bn_aggr(out=mv[sl][:], in_=st[sl][:]), [stb[sl]], [mvb[sl]], cost=0.15)
                        dve.op(lambda e: e.tensor_scalar(out=nb[sl][:], in0=mv[sl][:, 0:1], scalar1=-1.0, scalar2=None,
                                                         op0=ALU.mult), [mvb[sl]], [nbb[sl]], cost=0.15)
                        dve.op(lambda e: e.scalar_tensor_tensor(out=v[sl][:], in0=v[sl][:], scalar=nb[sl][:, 0:1],
                                                                in1=vgt[:], op0=ALU.add, op1=ALU.mult),
                               [vb_[sl], nbb[sl], cB], [vb_[sl]], cost=cd(2048))
                        rsqrt_pre(mv[sl][:, 1:2], mvb[sl], 1.0, EPS)
                        for fb in range(8, 12):
                            j = fb - 8
                            bank, bb = block(fb)
                            act.op(lambda e: e.activation(out=sz[sl][:, j * 512:(j + 1) * 512], in_=bank[:],
                                                          func=AF.Silu), [bb], [szb[sl]], waw=False, cost=ca(512))

                    def s3_tail(i):
                        sl = i % NS
                        rsqrt_post(rv[sl], rvb[sl])
                        dve.op(lambda e: e.scalar_tensor_tensor(out=vln[sl][:], in0=v[sl][:], scalar=rv[sl][:, 0:1],
                                                                in1=vbt[:], op0=ALU.mult, op1=ALU.add),
                               [vb_[sl], rvb[sl], cB], [vlnb[sl]], cost=cd(2048))
                        dve.op(lambda e: e.tensor_tensor(out=u[sl][:], in0=u[sl][:], in1=sz[sl][:], op=ALU.mult),
                               [ub[sl], szb[sl]], [ub[sl]], cost=cd(1024))

                    def s4(i):
                        sl = i % NS
                        for gp in range(4):
                            bank, bb = next_f()
                            for h in range(2):
                                g = 2 * gp + h
                                pe.op(lambda e: e.matmul(bank[:, h * 256:(h + 1) * 256], wsT[:, g, :],
                                                         vln[sl][:, g * 256:(g + 1) * 256], start=True, stop=True),
                                      [wsTb, vlnb[sl]], [bb], waw=False, inc=(h == 1))
                            for h in range(2):
                                g = 2 * gp + h
                                dve.op(lambda e: e.scalar_tensor_tensor(
                                    out=sz[sl][:, g * 256:(g + 1) * 256], in0=bank[:, h * 256:(h + 1) * 256],
                                    scalar=bsT[:, g:g + 1], in1=u[sl][:, g * 256:(g + 1) * 256],
                                    op0=ALU.add, op1=ALU.mult), [bb, cG, ub[sl]], [szb[sl]], waw=True, cost=cd(256))

                    def s5(i):
                        sl = i % NS
                        for hb in range(2):
                            bank, bb = next_b()
                            for j in range(8):
                                kc = hb * 8 + j
                                pe.op(lambda e: e.transpose(out=bank[:, j * 128:(j + 1) * 128],
                                                            in_=sz[sl][:, kc * 128:(kc + 1) * 128], identity=ident[:]),
                                      [szb[sl], identb], [bb], waw=False, inc=(j == 7))
                            copy_on(ev_engine(), g1t[sl][:, hb * 8:hb * 8 + 8, :],
                                    bank[:].rearrange("p (a b) -> p a b", b=128), [bb], [g1tb[sl][hb]], n=1024)

                    def s6(i):
                        sl = i % NS
                        banks = []
                        for nh in range(2):
                            bank, bb = next_f()
                            banks.append((bank, bb))
                            for kc in range(16):
                                pe.op(lambda e: e.matmul(bank[:], g1t[sl][:, kc, :], W1[:, kc, nh * 512:(nh + 1) * 512],
                                                         start=(kc == 0), stop=(kc == 15)),
                                      [g1tb[sl][kc // 8], W1b], [bb], waw=False, inc=(kc == 15))
                        for nh in range(2):
                            bank, bb = banks[nh]
                            dve.op(lambda e: e.tensor_tensor(out=x1[sl][:, nh * 512:(nh + 1) * 512],
                                                             in0=x1[sl][:, nh * 512:(nh + 1) * 512], in1=bank[:],
                                                             op=ALU.add), [bb, x1b[sl]], [x1b[sl]])
                        act.op(lambda e: e.activation(out=junk[:], in_=x1[sl][:], func=AF.Square,
                                                      accum_out=ss[sl][:]), [x1b[sl]], [junkb, ssb[sl]])
                        rstd_from_ss(ss[sl], rs[sl], ssb[sl], rsb[sl], D, EPS)
                        dve.op(lambda e: e.scalar_tensor_tensor(out=x1[sl][:], in0=x1[sl][:], scalar=rs[sl][:, 0:1],
                                                                in1=gFt[:], op0=ALU.mult, op1=ALU.mult),
                               [x1b[sl], rsb[sl], cB], [x1b[sl]])
                        sp.dma(out[i * 128:(i + 1) * 128, :], x1[sl][:], d_out, reads=[x1b[sl]], writes=[outb])

                    def load_h(i):
                        sp.dma(h1[i % NS][:], h1d[i], d_h[i % NS], reads=[h1db], writes=[h1b[i % NS]])

                    def load_x(i):
                        sp.dma(x1[i % NS][:], x1d[i * 128:(i + 1) * 128, :], d_x[i % NS], reads=[x1db],
                               writes=[x1b[i % NS]])

                    load_h(0)
                    for h in range(2):
                        pool.dma(W1[:, 8 * h:8 * h + 8, :],
                                 W1d[1024 * h:1024 * (h + 1), :].rearrange("(kc p) n -> p kc n", p=128),
                                 d_w, reads=[W1db], writes=[W1b])
                    load_x(0)
                    load_x(1)
                    for i in range(NT + 2):
                        if i + 1 < NT:
                            load_h(i + 1)
                        if i < NT:
                            s3(i)
                        if 0 <= i - 1 < NT:
                            s4(i - 1)
                        if i < NT:
                            s3_tail(i)
                        if 0 <= i - 2 < NT:
                            s6(i - 2)
                            if i < NT:
                                load_x(i)
                        if 0 <= i - 1 < NT:
                            s5(i - 1)
                barrier()
        barrier()
    return nc


_CACHE = {}


def _get(mode):
    if mode not in _CACHE:
        _CACHE[mode] = build(mode)
    return _CACHE[mode]


_WNAMES = ["l0_norm", "l0_w_in", "l0_w_out", "l1_norm", "l1_w_in", "l1_v_ln_g", "l1_v_ln_b", "l1_w_s", "l1_b_s",
           "l1_w_out", "final_norm"]


def _in_maps(inputs, extra=None):
    consts = _host_consts()
    x = np.ascontiguousarray(np.asarray(inputs["x"], dtype=np.float32))
    base = {k: np.ascontiguousarray(np.asarray(inputs[k], dtype=np.float32)) for k in _WNAMES}
    base.update(consts)
    maps = []
    for c in range(NCORES):
        m = dict(base)
        m["x"] = x[c]
        if extra is not None:
            for k, v in extra.items():
                m[k] = v[c]
        maps.append(m)
    return maps


def kernel(**inputs):
    nc = _get("AB")
    res = run_bass_kernel_spmd(nc, _in_maps(inputs), core_ids=list(range(NCORES)))
    return np.stack([r["out"] for r in res.results], axis=0).astype(np.float32)
```

```python
import numpy as np
import ml_dtypes
from contextlib import ExitStack
import concourse.bass as bass
import concourse.mybir as mybir
from concourse.bass_utils import run_bass_kernel_spmd

F32 = mybir.dt.float32
BF16 = mybir.dt.bfloat16
AF = mybir.ActivationFunctionType
ALU = mybir.AluOpType

S = 4096
D = 1024
E = 2048
NT = 32
EPS = 1e-6
NCORES = 8


def _host_consts():
    bf = ml_dtypes.bfloat16
    c = np.arange(256)[:, None]
    m = np.arange(256)[None, :]
    ang = 2 * np.pi * ((c * m) % 256) / 256.0
    Cc, Sc = np.cos(ang), np.sin(ang)
    FC = np.zeros((128, 2, 2, 256), np.float64)
    for kc in range(2):
        for mh in range(2):
            cs = slice(kc * 128, (kc + 1) * 128)
            ms = slice(mh * 128, (mh + 1) * 128)
            FC[:, kc, mh, 0:128] = Cc[cs, ms]
            FC[:, kc, mh, 128:256] = -Sc[cs, ms]
    p = np.arange(128)[:, None, None]
    t = np.arange(32)[None, :, None]
    kp = np.arange(128)[None, None, :]
    th = 2 * np.pi * ((32 * p * kp + t * kp) % 4096) / 4096.0
    WR = np.cos(th)
    WS = np.sin(th)
    tt = np.arange(32)[:, None]
    kt = np.arange(32)[None, :]
    ph = 2 * np.pi * ((tt * kt) % 32) / 32.0
    BDc = np.zeros((128, 128))
    BDs = np.zeros((128, 128))
    for q in range(4):
        BDc[32 * q:32 * q + 32, q::4] = np.cos(ph)
        BDs[32 * q:32 * q + 32, q::4] = np.sin(ph)
    ident = np.eye(128)
    return {
        "c_fc": FC.astype(np.float32).astype(bf),
        "c_wr": WR.astype(np.float32).astype(bf),
        "c_ws": WS.astype(np.float32).astype(bf),
        "c_wsn": (-WS).astype(np.float32).astype(bf),
        "c_bdc": BDc.astype(np.float32).astype(bf),
        "c_bds": BDs.astype(np.float32).astype(bf),
        "c_id": ident.astype(np.float32).astype(bf),
    }


class Buf:
    __slots__ = ("name", "w", "r")

    def __init__(self, name):
        self.name = name
        self.w = None
        self.r = {}


class DSem:
    def __init__(self, sem):
        self.sem = sem
        self.n = 0


class Q:
    def __init__(self, eng, sem, kind):
        self.eng = eng
        self.sem = sem
        self.n = 0
        self.seen = {}
        self.kind = kind
        self.load = 0.0

    def wait(self, tok):
        if tok is None:
            return
        sem, val = tok
        if sem is self.sem and self.kind == "pe":
            return
        k = id(sem)
        if self.seen.get(k, 0) >= val:
            return
        self.eng.wait_ge(sem, val)
        self.seen[k] = val

    def deps(self, reads, writes, waw=True):
        for b in reads:
            self.wait(b.w)
        for b in writes:
            if waw:
                self.wait(b.w)
            for t in list(b.r.values()):
                self.wait(t)

    def op(self, fn, reads=(), writes=(), waw=True, cost=0.3, inc=True):
        self.load += cost
        self.deps(reads, writes, waw)
        inst = fn(self.eng)
        if inc:
            self.n += 1
            inst.then_inc(self.sem, 1)
            tok = (self.sem, self.n)
        else:
            tok = (self.sem, self.n + 1)
        for b in reads:
            b.r[id(self.sem)] = tok
        for b in writes:
            b.w = tok
            b.r = {}
        return tok

    def dma(self, out, in_, dsem, reads=(), writes=(), **kw):
        self.deps(reads, writes, waw=False)
        self.eng.dma_start(out=out, in_=in_, **kw).then_inc(dsem.sem, 16)
        dsem.n += 16
        tok = (dsem.sem, dsem.n)
        for b in reads:
            b.r[id(dsem.sem)] = tok
        for b in writes:
            b.w = tok
            b.r = {}
        return tok


class Ctx:
    pass


def build(mode):
    nc = bass.Bass("TRN2", target_bir_lowering=False)
    K = Ctx()
    K.nc = nc
    dt = nc.dram_tensor

    def din(name, shape, dtype=F32):
        return dt(name, shape, dtype, kind="ExternalInput").ap()

    x = din("x", [S, D])
    g0 = din("l0_norm", [D])
    w_in0 = din("l0_w_in", [D, 2 * E])
    w_out0 = din("l0_w_out", [E, D])
    g1 = din("l1_norm", [D])
    w_in1 = din("l1_w_in", [D, 3 * E])
    vg = din("l1_v_ln_g", [E])
    vb = din("l1_v_ln_b", [E])
    w_s = din("l1_w_s", [8, 128, 128])
    b_s = din("l1_b_s", [8, 128])
    w_out1 = din("l1_w_out", [E, D])
    gF = din("final_norm", [D])
    c_fc = din("c_fc", [128, 2, 2, 256], BF16)
    c_wr = din("c_wr", [128, 32, 128], BF16)
    c_ws = din("c_ws", [128, 32, 128], BF16)
    c_wsn = din("c_wsn", [128, 32, 128], BF16)
    c_bdc = din("c_bdc", [128, 128], BF16)
    c_bds = din("c_bds", [128, 128], BF16)
    c_id = din("c_id", [128, 128], BF16)

    if mode == "A":
        gyT = dt("gy", [E, S], BF16, kind="ExternalOutput").ap()
    elif mode == "B":
        gyT = dt("gy", [E, S], BF16, kind="ExternalInput").ap()
    else:
        gyT = dt("gy", [E, S], BF16, kind="Internal").ap()
    if mode != "A":
        out = dt("out", [S, D], F32, kind="ExternalOutput").ap()
        x1d = dt("x1d", [S, D], F32, kind="Internal").ap()
        h1d = dt("h1d", [NT, 128, 8, 128], BF16, kind="Internal").ap()
    if mode != "B":
        Ud = dt("ud", [2, 128, 32, 256], BF16, kind="Internal").ap()
    if mode != "A":
        W0d = dt("w0d", [128, 16, D], BF16, kind="Internal").ap()
        Wind = dt("wind", [D, 3 * E], BF16, kind="Internal").ap()
        W1d = dt("w1d", [E, D], BF16, kind="Internal").ap()

    with ExitStack() as es:
        def sem(name):
            return es.enter_context(nc.semaphore(name))

        pe = Q(nc.tensor, sem("s_pe"), "pe")
        act = Q(nc.scalar, sem("s_act"), "act")
        dve = Q(nc.vector, sem("s_dve"), "dve")
        pool = Q(nc.gpsimd, sem("s_pool"), "pool")
        sp = Q(nc.sync, sem("s_sp"), "sp")
        queues = [pe, act, dve, pool, sp]
        dsems = []

        def dsem(name):
            d = DSem(sem(name))
            dsems.append(d)
            return d

        def barrier():
            for q in queues:
                for r in queues:
                    if r is not q and r.n > 0:
                        q.wait((r.sem, r.n))
                for d in dsems:
                    if d.n > 0:
                        q.wait((d.sem, d.n))

        pf = [es.enter_context(nc.psum_tensor(f"pf{i}", [128, 512], F32)) for i in range(6)]
        pb = [es.enter_context(nc.psum_tensor(f"pb{i}", [128, 1024], BF16)) for i in range(2)]
        pfb = [Buf(f"pf{i}") for i in range(6)]
        pbb = [Buf(f"pb{i}") for i in range(2)]
        cnt = {"f": 0, "b": 0}

        def next_f():
            i = cnt["f"] % 6
            cnt["f"] += 1
            return pf[i], pfb[i]

        def next_b():
            i = cnt["b"] % 2
            cnt["b"] += 1
            return pb[i], pbb[i]

        def ca(n):
            return 0.19 + n / 1200.0

        def cd(n):
            return 0.12 + n / 960.0

        def ev_engine():
            return act if act.load <= dve.load else dve

        def copy_on(q, out_ap, in_ap, reads, writes, n=512):
            if q is act:
                return q.op(lambda e: e.activation(out=out_ap, in_=in_ap, func=AF.Copy), reads, writes,
                            waw=False, cost=ca(n))
            return q.op(lambda e: e.tensor_copy(out=out_ap, in_=in_ap), reads, writes, waw=False, cost=cd(n))

        rtmp = es.enter_context(nc.sbuf_tensor("rtmp", [128, 2], F32))
        rtmpb = Buf("rtmp")

        def rsqrt_pre(src_ap, srcb, mul, eps):
            dve.op(lambda e: e.tensor_scalar(out=rtmp[:, 0:1], in0=src_ap, scalar1=mul, scalar2=eps,
                                             op0=ALU.mult, op1=ALU.add), [srcb], [rtmpb], cost=0.15)
            dve.op(lambda e: e.reciprocal(out=rtmp[:, 1:2], in_=rtmp[:, 0:1]), [rtmpb], [rtmpb], cost=0.15)

        def rsqrt_post(rstd_t, rstdb):
            act.op(lambda e: e.activation(out=rstd_t[:], in_=rtmp[:, 1:2], func=AF.Sqrt), [rtmpb], [rstdb], cost=1.5)

        def rsqrt_chain(src_ap, srcb, rstd_t, rstdb, mul, eps):
            rsqrt_pre(src_ap, srcb, mul, eps)
            rsqrt_post(rstd_t, rstdb)

        def rstd_from_ss(ss_t, rstd_t, ssb, rstdb, n, eps):
            rsqrt_chain(ss_t[:], ssb, rstd_t, rstdb, 1.0 / n, eps)

        ident = es.enter_context(nc.sbuf_tensor("ident", [128, 128], BF16))
        identb = Buf("ident")
        d_const = dsem("d_const")
        sp.dma(ident[:], c_id[:, :], d_const, writes=[identb])

        g0T = es.enter_context(nc.sbuf_tensor("g0T", [128, 8], F32))
        g1T = es.enter_context(nc.sbuf_tensor("g1T", [128, 8], F32))
        bsT = es.enter_context(nc.sbuf_tensor("bsT", [128, 8], F32))
        cG = Buf("constsG")
        d_cg = dsem("d_cg")
        pool.dma(g0T[:], g0.rearrange("(dc p) -> p dc", p=128), d_cg, writes=[cG], allow_slow_non_contiguous=True)
        pool.dma(g1T[:], g1.rearrange("(dc p) -> p dc", p=128), d_cg, writes=[cG], allow_slow_non_contiguous=True)
        pool.dma(bsT[:], b_s.rearrange("g p -> p g"), d_cg, writes=[cG], allow_slow_non_contiguous=True)

        if mode != "A":
            d_pc = dsem("d_pc")
            W0db = Buf("W0d")
            Windb = [Buf(f"Wind{i}") for i in range(12)]
            W1db = Buf("W1d")
            pc_order = [4, 5, 6, 7, 0, 1, 2, 3, 8, 9, 10, 11]

            def precast(i):
                if i < 2:
                    pool.dma(W0d[:, 8 * i:8 * i + 8, :],
                             w_out0[1024 * i:1024 * (i + 1), :].rearrange("(kc p) n -> p kc n", p=128), d_pc,
                             writes=[W0db])
                elif i < 14:
                    fb = pc_order[i - 2]
                    pool.dma(Wind[:, fb * 512:(fb + 1) * 512], w_in1[:, fb * 512:(fb + 1) * 512], d_pc,
                             writes=[Windb[fb]])
                else:
                    h = i - 14
                    pool.dma(W1d[1024 * h:1024 * (h + 1), :], w_out1[1024 * h:1024 * (h + 1), :], d_pc,
                             writes=[W1db])

        if mode != "B":
            with ExitStack() as sa:
                def sb(name, shape, dtype):
                    return sa.enter_context(nc.sbuf_tensor(name, shape, dtype))

                hT = sb("hT", [128, 8, S], BF16)
                hTb = Buf("hT")
                fc = sb("fc", [128, 2, 2, 256], BF16)
                wr = sb("wr", [128, 32, 128], BF16)
                ws = sb("ws", [128, 32, 128], BF16)
                wsneg = sb("wsneg", [128, 32, 128], BF16)
                bdc = sb("bdc", [128, 128], BF16)
                bds = sb("bds", [128, 128], BF16)
                cb = Buf("constsA")
                def load_consts_a():
                    sp.dma(fc[:], c_fc[:, :, :, :], d_const, writes=[cb])
                    sp.dma(wr[:], c_wr[:, :, :], d_const, writes=[cb])
                    sp.dma(ws[:], c_ws[:, :, :], d_const, writes=[cb])
                    sp.dma(wsneg[:], c_wsn[:, :, :], d_const, writes=[cb])
                    sp.dma(bdc[:], c_bdc[:, :], d_const, writes=[cb])
                    sp.dma(bds[:], c_bds[:, :], d_const, writes=[cb])
                    identb.w = (d_const.sem, d_const.n)
                    cb.w = (d_const.sem, d_const.n)

                wg = [sb(f"wg{i}", [128, 8, 512], BF16) for i in range(2)]
                wgb = [Buf(f"wg{i}") for i in range(2)]
                d_wg = [dsem(f"d_wg{i}") for i in range(2)]
                def load_wg(g):
                    sl = g % 2
                    pool.dma(wg[sl][:, :, 0:256],
                             w_in0[:, g * 256:(g + 1) * 256].rearrange("(dc p) n -> p dc n", p=128),
                             d_wg[sl], writes=[wgb[sl]])
                    pool.dma(wg[sl][:, :, 256:512],
                             w_in0[:, E + g * 256:E + (g + 1) * 256].rearrange("(dc p) n -> p dc n", p=128),
                             d_wg[sl], writes=[wgb[sl]])

                load_wg(0)

                with ExitStack() as s0:
                    NX = 22
                    xt = [s0.enter_context(nc.sbuf_tensor(f"a0x{i}", [128, D], F32)) for i in range(NX)]
                    xtb = [Buf(f"a0x{i}") for i in range(NX)]
                    d_x6 = [dsem(f"d_a0x{i}") for i in range(6)]
                    d_x = [d_x6[i % 6] for i in range(NX)]
                    junk = s0.enter_context(nc.sbuf_tensor("a0junk", [128, D], BF16))
                    junkb = Buf("junk")
                    xn = [s0.enter_context(nc.sbuf_tensor(f"a0xn{i}", [128, D], BF16)) for i in range(2)]
                    xnb = [Buf(f"a0xn{i}") for i in range(2)]
                    ss = [s0.enter_context(nc.sbuf_tensor(f"a0ss{i}", [128, 1], F32)) for i in range(2)]
                    ssb = [Buf(f"a0ss{i}") for i in range(2)]
                    rs = [s0.enter_context(nc.sbuf_tensor(f"a0rs{i}", [128, 1], F32)) for i in range(2)]
                    rsb = [Buf(f"a0rs{i}") for i in range(2)]
                    ssA = s0.enter_context(nc.sbuf_tensor("a0ssA", [128, NT], F32))
                    ssAb = Buf("ssA")
                    rsA = s0.enter_context(nc.sbuf_tensor("a0rsA", [128, NT], F32))
                    rsAb = Buf("rsA")
                    t32 = s0.enter_context(nc.sbuf_tensor("a0t32", [128, 2 * NT], F32))
                    t32b = Buf("t32")
                    def a0_load(ti, sx):
                        if d_x[sx].n > 0:
                            sp.wait((d_x[sx].sem, d_x[sx].n))
                        sp.dma(xt[sx][:], x[ti * 128:(ti + 1) * 128, :], d_x[sx], writes=[xtb[sx]])

                    def a0_sq(ti, sx):
                        act.op(lambda e: e.activation(out=junk[:], in_=xt[sx][:], func=AF.Square,
                                                      accum_out=ssA[:, ti:ti + 1]),
                               [xtb[sx]], [junkb, ssAb], waw=True, cost=ca(1024))

                    NE = NT - NX
                    for ti in range(NX, NT):
                        a0_load(ti, ti - NE)
                        a0_sq(ti, ti - NE)
                    for ti in range(0, NX):
                        a0_load(ti, ti)
                        a0_sq(ti, ti)
                    load_consts_a()
                    dve.op(lambda e: e.tensor_scalar(out=t32[:, 0:NT], in0=ssA[:], scalar1=1.0 / D, scalar2=EPS,
                                                     op0=ALU.mult, op1=ALU.add), [ssAb], [t32b])
                    dve.op(lambda e: e.reciprocal(out=t32[:, NT:2 * NT], in_=t32[:, 0:NT]), [t32b], [t32b])
                    act.op(lambda e: e.activation(out=rsA[:], in_=t32[:, NT:2 * NT], func=AF.Sqrt), [t32b], [rsAb])
                    for n2 in range(NT):
                        ti = n2
                        sx = ti if ti < NX else ti - NX
                        sl = n2 % 2
                        if True:
                            act.op(lambda e: e.activation(out=xn[sl][:], in_=xt[sx][:], func=AF.Copy,
                                                          scale=rsA[:, ti:ti + 1]),
                                   [xtb[sx], rsAb], [xnb[sl]], cost=ca(1024))
                        else:
                            dve.op(lambda e: e.tensor_scalar(out=xn[sl][:], in0=xt[sx][:],
                                                             scalar1=rsA[:, ti:ti + 1], scalar2=None,
                                                             op0=ALU.mult),
                                   [xtb[sx], rsAb], [xnb[sl]], cost=cd(512))
                        if ti < NE:
                            a0_load(NX + ti, ti)
                        bank, bb = next_b()
                        for dc in range(8):
                            pe.op(lambda e: e.transpose(out=bank[:, dc * 128:(dc + 1) * 128],
                                                        in_=xn[sl][:, dc * 128:(dc + 1) * 128],
                                                        identity=ident[:]),
                                  [xnb[sl], identb], [bb], waw=False, inc=(dc == 7))
                        dve.op(lambda e: e.tensor_tensor(
                            out=hT[:, :, ti * 128:(ti + 1) * 128],
                            in0=bank[:].rearrange("p (a b) -> p a b", b=128),
                            in1=g0T[:].rearrange("p (k o) -> p k o", o=1).broadcast_to([128, 8, 128]),
                            op=ALU.mult), [bb, cb, cG], [hTb], waw=False, cost=cd(1024))
                barrier()

                xinT = sb("xinT", [128, 2, S], BF16)
                xinTbs = [[Buf(f"xinT{kc}_{tb}") for tb in range(8)] for kc in range(2)]
                szT = sb("szT", [128, 2, S], BF16)
                szTb = Buf("szT")
                AB = sb("AB", [128, 32, 256], BF16)
                ABbs = [Buf(f"AB{t}") for t in range(16)]
                U = sb("U", [128, 32, 256], BF16)
                Ubs = [Buf(f"U{t}") for t in range(16)]
                V = sb("V", [128, 32, 256], BF16)
                Vb = Buf("V_sp")
                Vb2 = Buf("V_pool")
                gyU = sb("gyU", [128, S], BF16)
                gyUb = Buf("gyU")
                Udb = [Buf("Ud0"), Buf("Ud1")]
                d_ud = dsem("d_ud")
                d_v = dsem("d_v")
                d_v2 = dsem("d_v2")
                d_gy = dsem("d_gy")
                gyTb = Buf("gyT")

                def proj(g, which):
                    sl = g % 2
                    for kc in range(2):
                        for tb in range(8):
                            bank, bb = next_f()
                            for dc in range(8):
                                pe.op(lambda e: e.matmul(
                                    bank[:], wg[sl][:, dc, which * 256 + kc * 128: which * 256 + (kc + 1) * 128],
                                    hT[:, dc, tb * 512:(tb + 1) * 512], start=(dc == 0), stop=(dc == 7)),
                                    [wgb[sl], hTb], [bb], waw=False, inc=(dc == 7))
                            if which == 0:
                                copy_on(ev_engine(), xinT[:, kc, tb * 512:(tb + 1) * 512], bank[:], [bb],
                                        [xinTbs[kc][tb]])
                            else:
                                o = szT[:, kc, tb * 512:(tb + 1) * 512]
                                act.op(lambda e: e.activation(out=o, in_=bank[:], func=AF.Silu), [bb], [szTb],
                                       waw=False, cost=ca(512))

                def p3(g, mh):
                    for t2 in range(16):
                        bank, bb = next_f()
                        for h in range(2):
                            t = 2 * t2 + h
                            for kc in range(2):
                                lhsT = xinT[:, kc, :].rearrange("c (p t) -> c t p", t=32)[:, t, :]
                                pe.op(lambda e: e.matmul(bank[:, h * 256:(h + 1) * 256], lhsT, fc[:, kc, mh, :],
                                                         start=(kc == 0), stop=(kc == 1)),
                                      xinTbs[kc] + [cb], [bb], waw=False, inc=(h == 1 and kc == 1))
                        copy_on(ev_engine(), AB[:, 2 * t2:2 * t2 + 2, :],
                                bank[:].rearrange("p (a b) -> p a b", b=256), [bb], [ABbs[t2]])

                def p4(g, mh, slot):
                    for t2 in range(16):
                        bank, bb = next_f()
                        for h in range(2):
                            t = 2 * t2 + h
                            c0 = h * 256
                            pe.op(lambda e: e.matmul(bank[:, c0:c0 + 256], wr[:, t, :], AB[:, t, 0:256],
                                                     start=True, stop=False), [ABbs[t2], cb], [bb], waw=False, inc=(False))
                            pe.op(lambda e: e.matmul(bank[:, c0:c0 + 128], ws[:, t, :], AB[:, t, 128:256],
                                                     start=False, stop=False), [ABbs[t2], cb], [bb], waw=False, inc=(False))
                            pe.op(lambda e: e.matmul(bank[:, c0 + 128:c0 + 256], wsneg[:, t, :], AB[:, t, 0:128],
                                                     start=False, stop=True), [ABbs[t2], cb], [bb], waw=False, inc=(h == 1))
                        copy_on(ev_engine(), U[:, 2 * t2:2 * t2 + 2, :],
                                bank[:].rearrange("p (a b) -> p a b", b=256), [bb], [Ubs[t2]])
                    sp.dma(Ud[slot], U[:], d_ud, reads=Ubs, writes=[Udb[slot]])

                def load_v(slot):
                    for q in range(4):
                        if q % 2 == 0:
                            sp.dma(V[32 * q:32 * q + 32, :, :],
                                   Ud[slot].rearrange("(k q) t c -> q t k c", q=4)[q],
                                   d_v, reads=[Udb[slot]], writes=[Vb])
                        else:
                            pool.dma(V[32 * q:32 * q + 32, :, :],
                                     Ud[slot].rearrange("(k q) t c -> q t k c", q=4)[q],
                                     d_v2, reads=[Udb[slot]], writes=[Vb2])

                def p6p7(g, mh):
                    for b in range(8):
                        bank, bb = next_f()
                        for j in range(4):
                            kpp = 4 * b + j
                            o = bank[:].rearrange("p (kt j q) -> p j kt q", j=4, q=4)[:, j, :, :]
                            pe.op(lambda e: e.matmul(o, V[:, kpp, 0:128], bdc[:].rearrange("p (kt q) -> p kt q", q=4),
                                                     start=True, stop=False), [Vb, Vb2, cb], [bb], waw=False, inc=(False))
                            pe.op(lambda e: e.matmul(o, V[:, kpp, 128:256], bds[:].rearrange("p (kt q) -> p kt q", q=4),
                                                     start=False, stop=True), [Vb, Vb2, cb], [bb], waw=False, inc=(j == 3))

                        def tokview(t2d):
                            return t2d.rearrange("p (kt b r) -> p b kt r", b=8, r=16)[:, b, :, :]
                        dve.op(lambda e: e.scalar_tensor_tensor(
                            out=tokview(gyU[:]),
                            in0=bank[:].rearrange("p (kt r) -> p kt r", r=16),
                            scalar=1.0 / 1024.0,
                            in1=tokview(szT[:, mh, :]),
                            op0=ALU.mult, op1=ALU.mult), [bb, szTb], [gyUb], waw=False, cost=cd(512) + 0.2)
                    c0 = g * 256 + mh * 128
                    sp.dma(gyT[c0:c0 + 128, :], gyU[:], d_gy, reads=[gyUb], writes=[gyTb])

                for g in range(8):
                    if g + 1 < 8:
                        load_wg(g + 1)
                    if mode == "AB":
                        precast(2 * g)
                        precast(2 * g + 1)
                    proj(g, 0)
                    if g > 0:
                        p6p7(g - 1, 1)
                    p3(g, 0)
                    p4(g, 0, 0)
                    load_v(0)
                    proj(g, 1)
                    p6p7(g, 0)
                    p3(g, 1)
                    p4(g, 1, 1)
                    load_v(1)
                p6p7(7, 1)
                barrier()
        else:
            gyTb = Buf("gyT")

        if mode != "A":
            with ExitStack() as sB:
                def sb(name, shape, dtype):
                    return sB.enter_context(nc.sbuf_tensor(name, shape, dtype))

                Win = sb("Win", [128, 8, 3 * E], BF16)
                Winb = [Buf(f"Win{i}") for i in range(12)]
                gFt = sb("gFt", [128, D], F32)
                vgt = sb("vgt", [128, E], BF16)
                vbt = sb("vbt", [128, E], BF16)
                wsn = sb("wsn", [128, 8, 128], BF16)
                wsT = sb("wsT", [128, 8, 128], BF16)
                cB = Buf("constsB")
                wsnb = Buf("wsn")
                wsTb = Buf("wsT")
                d_cB = dsem("d_cB")
                d_w = dsem("d_w")
                d_w0 = dsem("d_w0")

                with ExitStack() as s0:
                    W0 = s0.enter_context(nc.sbuf_tensor("W0", [128, 16, D], BF16))
                    W0b = Buf("W0")
                    if mode == "B":
                        for i in range(16):
                            precast(i)
                    W0b2 = Buf("W0_pool")
                    d_w0b = dsem("d_w0b")
                    sp.dma(W0[:, 0:8, :], W0d[:, 0:8, :],
                           d_w0, reads=[W0db], writes=[W0b])
                    pool.dma(W0[:, 8:16, :], W0d[:, 8:16, :],
                             d_w0b, reads=[W0db], writes=[W0b2])

                    x1 = [s0.enter_context(nc.sbuf_tensor(f"b0x{i}", [128, D], F32)) for i in range(3)]
                    x1b = [Buf(f"b0x{i}") for i in range(3)]
                    d_x = [dsem(f"d_b0x{i}") for i in range(3)]
                    gyt = [s0.enter_context(nc.sbuf_tensor(f"b0gy{i}", [128, 16, 512], BF16)) for i in range(2)]
                    gytb = [Buf(f"b0gy{i}") for i in range(2)]
                    d_g = [dsem(f"d_b0g{i}") for i in range(2)]
                    xn = [s0.enter_context(nc.sbuf_tensor(f"b0xn{i}", [128, D], BF16)) for i in range(2)]
                    xnb = [Buf(f"b0xn{i}") for i in range(2)]
                    h1 = [s0.enter_context(nc.sbuf_tensor(f"b0h{i}", [128, 8, 128], BF16)) for i in range(2)]
                    h1b = [Buf(f"b0h{i}") for i in range(2)]
                    ss = [s0.enter_context(nc.sbuf_tensor(f"b0ss{i}", [128, 1], F32)) for i in range(2)]
                    ssb = [Buf(f"b0ss{i}") for i in range(2)]
                    rs = [s0.enter_context(nc.sbuf_tensor(f"b0rs{i}", [128, 1], F32)) for i in range(2)]
                    rsb = [Buf(f"b0rs{i}") for i in range(2)]
                    d_x1 = dsem("d_x1d")
                    d_g0b = dsem("d_g0b")
                    gy0hb = Buf("gy0_pool_half")
                    d_h1 = dsem("d_h1d")
                    x1db = Buf("x1d")
                    h1db = Buf("h1d")

                    def b0_load_gy(jb):
                        sl = jb % 2
                        src = gyT[:, jb * 512:(jb + 1) * 512].rearrange("(kc p) s -> p kc s", p=128)
                        if jb == 0:
                            sp.dma(gyt[sl][:, 0:8, :], src[:, 0:8, :], d_g[sl], reads=[gyTb], writes=[gytb[sl]])
                            pool.dma(gyt[sl][:, 8:16, :], src[:, 8:16, :], d_g0b, reads=[gyTb], writes=[gy0hb])
                        else:
                            sp.dma(gyt[sl][:], src, d_g[sl], reads=[gyTb], writes=[gytb[sl]])

                    def b0_load_x(i):
                        sp.dma(x1[i % 3][:], x[i * 128:(i + 1) * 128, :], d_x[i % 3], writes=[x1b[i % 3]])

                    def b0_front(i):
                        sl = i % 2
                        s3_ = i % 3
                        jb = i // 4
                        gt = gyt[jb % 2]
                        gb = gytb[jb % 2]
                        gbs = [gb, gy0hb] if jb == 0 else [gb]
                        c0 = (i % 4) * 128
                        banks = []
                        for nh in range(2):
                            bank, bb = next_f()
                            banks.append((bank, bb))
                            for kc in range(16):
                                pe.op(lambda e: e.matmul(bank[:], gt[:, kc, c0:c0 + 128],
                                                         W0[:, kc, nh * 512:(nh + 1) * 512],
                                                         start=(kc == 0), stop=(kc == 15)),
                                      gbs + [W0b, W0b2], [bb], waw=False, inc=(kc == 15))
                        for nh in range(2):
                            bank, bb = banks[nh]
                            dve.op(lambda e: e.tensor_tensor(out=x1[s3_][:, nh * 512:(nh + 1) * 512],
                                                             in0=x1[s3_][:, nh * 512:(nh + 1) * 512], in1=bank[:],
                                                             op=ALU.add), [bb, x1b[s3_]], [x1b[s3_]], cost=cd(512))
                        act.op(lambda e: e.activation(out=xn[sl][:], in_=x1[s3_][:], func=AF.Square,
                                                      accum_out=ss[sl][:]), [x1b[s3_]], [xnb[sl], ssb[sl]],
                               cost=ca(1024))
                        rstd_from_ss(ss[sl], rs[sl], ssb[sl], rsb[sl], D, EPS)
                        act.op(lambda e: e.activation(out=xn[sl][:], in_=x1[s3_][:], func=AF.Copy,
                                                      scale=rs[sl][:, 0:1]), [x1b[s3_], rsb[sl]], [xnb[sl]],
                               cost=ca(1024))

                    def b0_back(i):
                        sl = i % 2
                        s3_ = i % 3
                        bank, bb = next_b()
                        for dc in range(8):
                            pe.op(lambda e: e.transpose(out=bank[:, dc * 128:(dc + 1) * 128],
                                                        in_=xn[sl][:, dc * 128:(dc + 1) * 128], identity=ident[:]),
                                  [xnb[sl], identb], [bb], waw=False, inc=(dc == 7))
                        dve.op(lambda e: e.tensor_tensor(
                            out=h1[sl][:], in0=bank[:].rearrange("p (a b) -> p a b", b=128),
                            in1=g1T[:].rearrange("p (k o) -> p k o", o=1).broadcast_to([128, 8, 128]),
                            op=ALU.mult), [bb, cG], [h1b[sl]], cost=cd(1024))
                        sp.dma(x1d[i * 128:(i + 1) * 128, :], x1[s3_][:], d_x1, reads=[x1b[s3_]], writes=[x1db])
                        sp.dma(h1d[i], h1[sl][:], d_h1, reads=[h1b[sl]], writes=[h1db])

                    b0_load_x(0)
                    b0_load_gy(0)
                    pool.dma(gFt[:], gF.partition_broadcast(128), d_cB, writes=[cB])
                    pool.dma(vgt[:], vg.partition_broadcast(128), d_cB, writes=[cB])
                    pool.dma(vbt[:], vb.partition_broadcast(128), d_cB, writes=[cB])
                    pool.dma(wsn[:], w_s.rearrange("g p q -> p g q"), d_cB, writes=[wsnb])
                    for fb in pc_order:
                        pool.dma(Win[:, :, fb * 512:(fb + 1) * 512],
                                 Wind[:, fb * 512:(fb + 1) * 512].rearrange("(kc p) n -> p kc n", p=128),
                                 d_w, reads=[Windb[fb]], writes=[Winb[fb]])
                    for i in range(NT + 1):
                        if i < NT:
                            if i % 4 == 0 and i // 4 + 1 < NT // 4:
                                b0_load_gy(i // 4 + 1)
                            if i + 1 < NT:
                                b0_load_x(i + 1)
                            b0_front(i)
                        if i >= 1:
                            b0_back(i - 1)
                    bank, bb = next_b()
                    for g in range(8):
                        pe.op(lambda e: e.transpose(out=bank[:, g * 128:(g + 1) * 128], in_=wsn[:, g, :],
                                                    identity=ident[:]), [wsnb, identb], [bb], waw=False, inc=(g == 7))
                    dve.op(lambda e: e.tensor_copy(out=wsT[:], in_=bank[:].rearrange("p (a b) -> p a b", b=128)),
                           [bb], [wsTb])
                barrier()

                with ExitStack() as s1:
                    def sb1(name, shape, dtype):
                        return s1.enter_context(nc.sbuf_tensor(name, shape, dtype))
                    NS = 2
                    W1 = sb1("W1", [128, 16, D], BF16)
                    W1b = Buf("W1")
                    x1 = [sb1(f"b1x{i}", [128, D], F32) for i in range(NS)]
                    x1b = [Buf(f"b1x{i}") for i in range(NS)]
                    d_x = [dsem(f"d_b1x{i}") for i in range(NS)]
                    h1 = [sb1(f"b1h{i}", [128, 8, 128], BF16) for i in range(NS)]
                    h1b = [Buf(f"b1h{i}") for i in range(NS)]
                    d_h = [dsem(f"d_b1h{i}") for i in range(NS)]
                    u = [sb1(f"b1u{i}", [128, E], BF16) for i in range(NS)]
                    ub = [Buf(f"b1u{i}") for i in range(NS)]
                    v = [sb1("b1v0", [128, E], F32)] * NS
                    vb_ = [Buf("b1v0")] * NS
                    sz = [sb1(f"b1sz{i}", [128, E], BF16) for i in range(NS)]
                    szb = [Buf(f"b1sz{i}") for i in range(NS)]
                    vln = [sb1(f"b1vln{i}", [128, E], BF16) for i in range(NS)]
                    vlnb = [Buf(f"b1vln{i}") for i in range(NS)]
                    g1t = [sb1(f"b1g1T{i}", [128, 16, 128], BF16) for i in range(NS)]
                    g1tb = [[Buf(f"b1g1T{i}_{h}") for h in range(2)] for i in range(NS)]
                    junk = sb1("b1junk", [128, D], BF16)
                    junkb = Buf("junk")
                    st = [sb1(f"b1st{i}", [128, 24], F32) for i in range(NS)]
                    stb = [Buf(f"b1st{i}") for i in range(NS)]
                    mv = [sb1(f"b1mv{i}", [128, 2], F32) for i in range(NS)]
                    mvb = [Buf(f"b1mv{i}") for i in range(NS)]
                    rv = [sb1(f"b1rv{i}", [128, 1], F32) for i in range(NS)]
                    rvb = [Buf(f"b1rv{i}") for i in range(NS)]
                    nb = [sb1(f"b1nb{i}", [128, 1], F32) for i in range(NS)]
                    nbb = [Buf(f"b1nb{i}") for i in range(NS)]
                    ss = [sb1(f"b1ss{i}", [128, 1], F32) for i in range(NS)]
                    ssb = [Buf(f"b1ss{i}") for i in range(NS)]
                    rs = [sb1(f"b1rs{i}", [128, 1], F32) for i in range(NS)]
                    rsb = [Buf(f"b1rs{i}") for i in range(NS)]
                    d_out = dsem("d_out")
                    outb = Buf("out")

                    def s3(i):
                        sl = i % NS

                        def block(fb):
                            bank, bb = next_f()
                            for kc in range(8):
                                pe.op(lambda e: e.matmul(bank[:], h1[sl][:, kc, :], Win[:, kc, fb * 512:(fb + 1) * 512],
                                                         start=(kc == 0), stop=(kc == 7)),
                                      [h1b[sl], Winb[fb]], [bb], waw=False, inc=(kc == 7))
                            return bank, bb

                        for fb in range(0, 4):
                            bank, bb = block(fb)
                            act.op(lambda e: e.activation(out=u[sl][:, fb * 512:(fb + 1) * 512], in_=bank[:],
                                                          func=AF.Gelu_apprx_tanh), [bb], [ub[sl]], waw=False,
                                   cost=ca(512))
                        for fb in range(4, 8):
                            j = fb - 4
                            bank, bb = block(fb)
                            act.op(lambda e: e.activation(out=v[sl][:, j * 512:(j + 1) * 512], in_=bank[:],
                                                          func=AF.Gelu_apprx_tanh), [bb], [vb_[sl]], waw=False,
                                   cost=ca(512))
                            dve.op(lambda e: e.bn_stats(out=st[sl][:, j * 6:(j + 1) * 6],
                                                        in_=v[sl][:, j * 512:(j + 1) * 512]),
                                   [vb_[sl]], [stb[sl]], waw=False, cost=cd(512))
                        dve.op(lambda e: e.bn_aggr(out=mv[sl][:], in_=st[sl][:]), [stb[sl]], [mvb[sl]], cost=0.15)
                        dve.op(lambda e: e.tensor_scalar(out=nb[sl][:], in0=mv[sl][:, 0:1], scalar1=-1.0, scalar2=None,
                                                         op0=ALU.mult), [mvb[sl]], [nbb[sl]], cost=0.15)
                        dve.op(lambda e: e.scalar_tensor_tensor(out=v[sl][:], in0=v[sl][:], scalar=nb[sl][:, 0:1],
                                                                in1=vgt[:], op0=ALU.add, op1=ALU.mult),
                               [vb_[sl], nbb[sl], cB], [vb_[sl]], cost=cd(2048))
                        rsqrt_pre(mv[sl][:, 1:2], mvb[sl], 1.0, EPS)
                        for fb in range(8, 12):
                            j = fb - 8
                            bank, bb = block(fb)
                            act.op(lambda e: e.activation(out=sz[sl][:, j * 512:(j + 1) * 512], in_=bank[:],
                                                          func=AF.Silu), [bb], [szb[sl]], waw=False, cost=ca(512))

                    def s3_tail(i):
                        sl = i % NS
                        rsqrt_post(rv[sl], rvb[sl])
                        dve.op(lambda e: e.scalar_tensor_tensor(out=vln[sl][:], in0=v[sl][:], scalar=rv[sl][:, 0:1],
                                                                in1=vbt[:], op0=ALU.mult, op1=ALU.add),
                               [vb_[sl], rvb[sl], cB], [vlnb[sl]], cost=cd(2048))
                        dve.op(lambda e: e.tensor_tensor(out=u[sl][:], in0=u[sl][:], in1=sz[sl][:], op=ALU.mult),
                               [ub[sl], szb[sl]], [ub[sl]], cost=cd(1024))

                    def s4(i):
                        sl = i % NS
                        for gp in range(4):
                            bank, bb = next_f()
                            for h in range(2):
                                g = 2 * gp + h
                                pe.op(lambda e: e.matmul(bank[:, h * 256:(h + 1) * 256], wsT[:, g, :],
                                                         vln[sl][:, g * 256:(g + 1) * 256], start=True, stop=True),
                                      [wsTb, vlnb[sl]], [bb], waw=False, inc=(h == 1))
                            for h in range(2):
                                g = 2 * gp + h
                                dve.op(lambda e: e.scalar_tensor_tensor(
                                    out=sz[sl][:, g * 256:(g + 1) * 256], in0=bank[:, h * 256:(h + 1) * 256],
                                    scalar=bsT[:, g:g + 1], in1=u[sl][:, g * 256:(g + 1) * 256],
                                    op0=ALU.add, op1=ALU.mult), [bb, cG, ub[sl]], [szb[sl]], waw=True, cost=cd(256))

                    def s5(i):
                        sl = i % NS
                        for hb in range(2):
                            bank, bb = next_b()
                            for j in range(8):
                                kc = hb * 8 + j
                                pe.op(lambda e: e.transpose(out=bank[:, j * 128:(j + 1) * 128],
                                                            in_=sz[sl][:, kc * 128:(kc + 1) * 128], identity=ident[:]),
                                      [szb[sl], identb], [bb], waw=False, inc=(j == 7))
                            copy_on(ev_engine(), g1t[sl][:, hb * 8:hb * 8 + 8, :],
                                    bank[:].rearrange("p (a b) -> p a b", b=128), [bb], [g1tb[sl][hb]], n=1024)

                    def s6(i):
                        sl = i % NS
                        banks = []
                        for nh in range(2):
                            bank, bb = next_f()
                            banks.append((bank, bb))
                            for kc in range(16):
                                pe.op(lambda e: e.matmul(bank[:], g1t[sl][:, kc, :], W1[:, kc, nh * 512:(nh + 1) * 512],
                                                         start=(kc == 0), stop=(kc == 15)),
                                      [g1tb[sl][kc // 8], W1b], [bb], waw=False, inc=(kc == 15))
                        for nh in range(2):
                            bank, bb = banks[nh]
                            dve.op(lambda e: e.tensor_tensor(out=x1[sl][:, nh * 512:(nh + 1) * 512],
                                                             in0=x1[sl][:, nh * 512:(nh + 1) * 512], in1=bank[:],
                                                             op=ALU.add), [bb, x1b[sl]], [x1b[sl]])
                        act.op(lambda e: e.activation(out=junk[:], in_=x1[sl][:], func=AF.Square,
                                                      accum_out=ss[sl][:]), [x1b[sl]], [junkb, ssb[sl]])
                        rstd_from_ss(ss[sl], rs[sl], ssb[sl], rsb[sl], D, EPS)
                        dve.op(lambda e: e.scalar_tensor_tensor(out=x1[sl][:], in0=x1[sl][:], scalar=rs[sl][:, 0:1],
                                                                in1=gFt[:], op0=ALU.mult, op1=ALU.mult),
                               [x1b[sl], rsb[sl], cB], [x1b[sl]])
                        sp.dma(out[i * 128:(i + 1) * 128, :], x1[sl][:], d_out, reads=[x1b[sl]], writes=[outb])

                    def load_h(i):
                        sp.dma(h1[i % NS][:], h1d[i], d_h[i % NS], reads=[h1db], writes=[h1b[i % NS]])

                    def load_x(i):
                        sp.dma(x1[i % NS][:], x1d[i * 128:(i + 1) * 128, :], d_x[i % NS], reads=[x1db],
                               writes=[x1b[i % NS]])

                    load_h(0)
                    for h in range(2):
                        pool.dma(W1[:, 8 * h:8 * h + 8, :],
                                 W1d[1024 * h:1024 * (h + 1), :].rearrange("(kc p) n -> p kc n", p=128),
                                 d_w, reads=[W1db], writes=[W1b])
                    load_x(0)
                    load_x(1)
                    for i in range(NT + 2):
                        if i + 1 < NT:
                            load_h(i + 1)
                        if i < NT:
                            s3(i)
                        if 0 <= i - 1 < NT:
                            s4(i - 1)
                        if i < NT:
                            s3_tail(i)
                        if 0 <= i - 2 < NT:
                            s6(i - 2)
                            if i < NT:
                                load_x(i)
                        if 0 <= i - 1 < NT:
                            s5(i - 1)
                barrier()
        barrier()
    return nc


_CACHE = {}


def _get(mode):
    if mode not in _CACHE:
        _CACHE[mode] = build(mode)
    return _CACHE[mode]


_WNAMES = ["l0_norm", "l0_w_in", "l0_w_out", "l1_norm", "l1_w_in", "l1_v_ln_g", "l1_v_ln_b", "l1_w_s", "l1_b_s",
           "l1_w_out", "final_norm"]


def _in_maps(inputs, extra=None):
    consts = _host_consts()
    x = np.ascontiguousarray(np.asarray(inputs["x"], dtype=np.float32))
    base = {k: np.ascontiguousarray(np.asarray(inputs[k], dtype=np.float32)) for k in _WNAMES}
    base.update(consts)
    maps = []
    for c in range(NCORES):
        m = dict(base)
        m["x"] = x[c]
        if extra is not None:
            for k, v in extra.items():
                m[k] = v[c]
        maps.append(m)
    return maps


def kernel(**inputs):
    nc = _get("AB")
    res = run_bass_kernel_spmd(nc, _in_maps(inputs), core_ids=list(range(NCORES)))
    return np.stack([r["out"] for r in res.results], axis=0).astype(np.float32)
```
